# Optimizing a Trainium2 kernel written in Bass

```python
import math
import jax
import jax.numpy as jnp
from jax import lax
import numpy as np

D_MODEL = 1024
BATCH = 4
SEQ = 4096
DEPTH = 1

D_MIX = D_MODEL
D_CONV = D_MIX // 2
D_SSM = D_MIX - D_CONV
CONV_WIDTH = 31
SSM_GROUP = 16
SSM_GROUPS = D_SSM // SSM_GROUP
SSM_STATE = 64
OUT_HEAD_DIM = 64
D_IN = 2 * D_CONV + D_SSM
D_FF = ((8 * D_MODEL // 3 + 127) // 128) * 128
FFN_RES = 0.5
EPS = 1e-6
DT_MIN = 1e-3
DT_MAX = 1e-1

kernel_name = 'hybrid_conformer_s5_encoder_layer'


def rms_norm(x, g):
    xf = x.astype(jnp.float32)
    y = xf * lax.rsqrt(jnp.mean(xf * xf, axis=-1, keepdims=True) + EPS)
    return (y * g.astype(jnp.float32)).astype(x.dtype)


def head_rms_norm(x, g, head_dim):
    b, l, d = x.shape
    xh = x.astype(jnp.float32).reshape(b, l, d // head_dim, head_dim)
    y = xh * lax.rsqrt(jnp.mean(xh * xh, axis=-1, keepdims=True) + EPS)
    return (y.reshape(b, l, d) * g.astype(jnp.float32)).astype(x.dtype)


def layer_norm(x, g, b):
    xf = x.astype(jnp.float32)
    mu = jnp.mean(xf, axis=-1, keepdims=True)
    var = jnp.mean(jnp.square(xf - mu), axis=-1, keepdims=True)
    y = (xf - mu) * lax.rsqrt(var + EPS)
    return (y * g.astype(jnp.float32) + b.astype(jnp.float32)).astype(x.dtype)


def swiglu(u, w_gate, w_up, w_down):
    return (jax.nn.silu(u @ w_gate) * (u @ w_up)) @ w_down


def depthwise_conv(v, w, b):
    rhs = w[:, None, :]
    pad = CONV_WIDTH // 2
    y = lax.conv_general_dilated(
        v, rhs, window_strides=(1,), padding=[(pad, pad)],
        dimension_numbers=('NWC', 'WIO', 'NWC'),
        feature_group_count=v.shape[-1])
    return y + b


def _linear_recurrence(e1, e2):
    a1, b1 = e1
    a2, b2 = e2
    return a1 * a2, a2 * b1 + b2


def s5_direction(u, lam_re, lam_im, log_step, b_re, b_im, c_re, c_im, reverse):
    lam = lax.complex(lam_re.astype(jnp.float32), lam_im.astype(jnp.float32))
    step = jnp.exp(log_step.astype(jnp.float32))[:, None]
    a_bar = jnp.exp(lam * step)
    b_c = lax.complex(b_re.astype(jnp.float32), b_im.astype(jnp.float32))
    b_bar = ((a_bar - 1.0) / lam)[..., None] * b_c
    bu = jnp.einsum('blgh,gph->blgp', u.astype(jnp.complex64), b_bar)
    a = jnp.broadcast_to(a_bar, bu.shape)
    _, states = lax.associative_scan(_linear_recurrence, (a, bu), axis=1, reverse=reverse)
    c_c = lax.complex(c_re.astype(jnp.float32), c_im.astype(jnp.float32))
    return jnp.real(jnp.einsum('blgp,ghp->blgh', states, c_c))


def hybrid_mixer(u, w_in, b_in, conv_w, conv_b, conv_ln_g, conv_ln_b, conv_out_g,
                 lam_re_f, lam_im_f, log_step_f, b_re_f, b_im_f, c_re_f, c_im_f,
                 lam_re_b, lam_im_b, log_step_b, b_re_b, b_im_b, c_re_b, c_im_b,
                 ssm_d, ssm_glu_w, ssm_glu_b, ssm_out_g, w_out, b_out):
    bsz, seq, _ = u.shape
    z = u @ w_in + b_in
    conv_val = z[..., :D_CONV]
    conv_gate = z[..., D_CONV:2 * D_CONV]
    ssm_in = z[..., 2 * D_CONV:]

    g = conv_val * jax.nn.sigmoid(conv_gate)
    d = depthwise_conv(g, conv_w, conv_b)
    d = jax.nn.silu(layer_norm(d, conv_ln_g, conv_ln_b))
    conv_y = head_rms_norm(d, conv_out_g, OUT_HEAD_DIM)

    s = ssm_in.astype(jnp.float32).reshape(bsz, seq, SSM_GROUPS, SSM_GROUP)
    y = (s5_direction(s, lam_re_f, lam_im_f, log_step_f, b_re_f, b_im_f, c_re_f, c_im_f, False)
         + s5_direction(s, lam_re_b, lam_im_b, log_step_b, b_re_b, b_im_b, c_re_b, c_im_b, True)
         + s * ssm_d.astype(jnp.float32).reshape(SSM_GROUPS, SSM_GROUP))
    y = jax.nn.gelu(y.reshape(bsz, seq, D_SSM).astype(u.dtype))
    y = y * jax.nn.sigmoid(y @ ssm_glu_w + ssm_glu_b)
    ssm_y = head_rms_norm(y, ssm_out_g, OUT_HEAD_DIM)

    return jnp.concatenate([conv_y, ssm_y], axis=-1) @ w_out + b_out


def setup_inputs(seed: int = 0) -> dict:
    key = jax.random.key(seed)
    ks = iter(jax.random.split(key, 64))
    f32 = jnp.float32

    def normal(shape, scale):
        return jax.random.normal(next(ks), shape, f32) * scale

    def gain(shape):
        return 1.0 + normal(shape, 0.05)

    def ssm_dir():
        lam_re = -0.5 + normal((DEPTH, SSM_GROUPS, SSM_STATE), 0.01)
        lam_im = jnp.pi * jnp.arange(SSM_STATE, dtype=f32) + normal((DEPTH, SSM_GROUPS, SSM_STATE), 0.01)
        log_step = jax.random.uniform(next(ks), (DEPTH, SSM_GROUPS), f32,
                                      minval=math.log(DT_MIN), maxval=math.log(DT_MAX))
        b_re = normal((DEPTH, SSM_GROUPS, SSM_STATE, SSM_GROUP), (2 * SSM_GROUP) ** -0.5)
        b_im = normal((DEPTH, SSM_GROUPS, SSM_STATE, SSM_GROUP), (2 * SSM_GROUP) ** -0.5)
        c_re = normal((DEPTH, SSM_GROUPS, SSM_GROUP, SSM_STATE), SSM_STATE ** -0.5)
        c_im = normal((DEPTH, SSM_GROUPS, SSM_GROUP, SSM_STATE), SSM_STATE ** -0.5)
        return lam_re, lam_im, log_step, b_re, b_im, c_re, c_im

    x = normal((BATCH, SEQ, D_MODEL), 1.0)
    inp = {}
    inp['x'] = x
    inp['ffn1_pre_g'] = gain((DEPTH, D_MODEL))
    inp['ffn1_w_gate'] = normal((DEPTH, D_MODEL, D_FF), D_MODEL ** -0.5)
    inp['ffn1_w_up'] = normal((DEPTH, D_MODEL, D_FF), D_MODEL ** -0.5)
    inp['ffn1_w_down'] = normal((DEPTH, D_FF, D_MODEL), D_FF ** -0.5)
    inp['ffn1_post_g'] = gain((DEPTH, D_MODEL))
    inp['mix_pre_g'] = gain((DEPTH, D_MODEL))
    inp['w_in'] = normal((DEPTH, D_MODEL, D_IN), D_MODEL ** -0.5)
    inp['b_in'] = normal((DEPTH, D_IN), 0.02)
    inp['conv_w'] = normal((DEPTH, CONV_WIDTH, D_CONV), CONV_WIDTH ** -0.5)
    inp['conv_b'] = normal((DEPTH, D_CONV), 0.02)
    inp['conv_ln_g'] = gain((DEPTH, D_CONV))
    inp['conv_ln_b'] = normal((DEPTH, D_CONV), 0.02)
    inp['conv_out_g'] = gain((DEPTH, D_CONV))
    (inp['lam_re_f'], inp['lam_im_f'], inp['log_step_f'], inp['b_re_f'],
     inp['b_im_f'], inp['c_re_f'], inp['c_im_f']) = ssm_dir()
    (inp['lam_re_b'], inp['lam_im_b'], inp['log_step_b'], inp['b_re_b'],
     inp['b_im_b'], inp['c_re_b'], inp['c_im_b']) = ssm_dir()
    inp['ssm_d'] = normal((DEPTH, D_SSM), 1.0)
    inp['ssm_glu_w'] = normal((DEPTH, D_SSM, D_SSM), D_SSM ** -0.5)
    inp['ssm_glu_b'] = normal((DEPTH, D_SSM), 0.02)
    inp['ssm_out_g'] = gain((DEPTH, D_SSM))
    inp['w_out'] = normal((DEPTH, D_MIX, D_MODEL), D_MIX ** -0.5)
    inp['b_out'] = normal((DEPTH, D_MODEL), 0.02)
    inp['mix_post_g'] = gain((DEPTH, D_MODEL))
    inp['ffn2_pre_g'] = gain((DEPTH, D_MODEL))
    inp['ffn2_w_gate'] = normal((DEPTH, D_MODEL, D_FF), D_MODEL ** -0.5)
    inp['ffn2_w_up'] = normal((DEPTH, D_MODEL, D_FF), D_MODEL ** -0.5)
    inp['ffn2_w_down'] = normal((DEPTH, D_FF, D_MODEL), D_FF ** -0.5)
    inp['ffn2_post_g'] = gain((DEPTH, D_MODEL))
    return inp


def reference(x, ffn1_pre_g, ffn1_w_gate, ffn1_w_up, ffn1_w_down, ffn1_post_g,
              mix_pre_g, w_in, b_in, conv_w, conv_b, conv_ln_g, conv_ln_b, conv_out_g,
              lam_re_f, lam_im_f, log_step_f, b_re_f, b_im_f, c_re_f, c_im_f,
              lam_re_b, lam_im_b, log_step_b, b_re_b, b_im_b, c_re_b, c_im_b,
              ssm_d, ssm_glu_w, ssm_glu_b, ssm_out_g, w_out, b_out, mix_post_g,
              ffn2_pre_g, ffn2_w_gate, ffn2_w_up, ffn2_w_down, ffn2_post_g):
    h = x
    for l in range(DEPTH):
        f = swiglu(rms_norm(h, ffn1_pre_g[l]), ffn1_w_gate[l], ffn1_w_up[l], ffn1_w_down[l])
        h = h + FFN_RES * rms_norm(f, ffn1_post_g[l])
        m = hybrid_mixer(rms_norm(h, mix_pre_g[l]), w_in[l], b_in[l],
                         conv_w[l], conv_b[l], conv_ln_g[l], conv_ln_b[l], conv_out_g[l],
                         lam_re_f[l], lam_im_f[l], log_step_f[l], b_re_f[l], b_im_f[l], c_re_f[l], c_im_f[l],
                         lam_re_b[l], lam_im_b[l], log_step_b[l], b_re_b[l], b_im_b[l], c_re_b[l], c_im_b[l],
                         ssm_d[l], ssm_glu_w[l], ssm_glu_b[l], ssm_out_g[l], w_out[l], b_out[l])
        h = h + rms_norm(m, mix_post_g[l])
        f = swiglu(rms_norm(h, ffn2_pre_g[l]), ffn2_w_gate[l], ffn2_w_up[l], ffn2_w_down[l])
        h = h + FFN_RES * rms_norm(f, ffn2_post_g[l])
    return h
```

```python
import math
from contextlib import ExitStack
import numpy as np
import concourse.bass as bass
import concourse.mybir as mybir
from concourse.bass_utils import run_bass_kernel_spmd

F32 = mybir.dt.float32
BF16 = mybir.dt.bfloat16
ALU = mybir.AluOpType
AF = mybir.ActivationFunctionType

NCORES = 8
D = 1024
DFF = 2816
NF = DFF // 128
TOK = 2048
CH = 512
NCH = TOK // CH
SC = 1024
NSC = TOK // SC
NS = SC // CH
NB = TOK // 8
EPS = 1e-6
PI = math.pi
ENG = ("pe", "act", "dve", "pool", "sp")
DEBUG = None


class Prog:
    def __init__(self):
        self.ops = {e: [] for e in ENG}
        self.last_w = {}
        self.readers = {}
        self.dma_n = {}
        self.dma_last = {}
        self.bar = {e: set() for e in ENG}

    def add(self, eng, fn, reads=(), writes=(), dma=None):
        deps = set(self.bar[eng])
        self.bar[eng] = set()
        for r in reads:
            t = self.last_w.get(r)
            if t is not None:
                deps.add(t)
        for w in writes:
            t = self.last_w.get(w)
            if t is not None:
                deps.add(t)
            for t in self.readers.get(w, {}).values():
                deps.add(t)
        if dma is not None:
            t = self.dma_last.get(dma)
            if t is not None:
                deps.add(t)
            n = self.dma_n.get(dma, 0) + 1
            self.dma_n[dma] = n
            tok = ("d", dma, n)
            self.dma_last[dma] = tok
        else:
            tok = ("c", eng, len(self.ops[eng]))
        self.ops[eng].append(dict(fn=fn, deps=deps, dma=dma, tok=tok, sig=False, sv=0))
        for w in writes:
            self.last_w[w] = tok
            self.readers[w] = {}
        for r in reads:
            self.readers.setdefault(r, {})[tok[:2]] = tok
        return tok

    def barrier(self):
        toks = set()
        for e in ENG:
            if self.ops[e]:
                o = self.ops[e][-1]
                if o["dma"] is None:
                    toks.add(o["tok"])
        toks.update(self.dma_last.values())
        for e in ENG:
            self.bar[e] |= toks

    def emit(self, nc, stack):
        for e in ENG:
            widx = {}
            for op in self.ops[e]:
                cw = []
                for t in op["deps"]:
                    if t[0] == "c":
                        if t[1] == "pe" and e == "pe":
                            continue
                        if t[2] > widx.get(t[1], -1):
                            cw.append(t)
                best = {}
                for t in cw:
                    if t[2] > best.get(t[1], -1):
                        best[t[1]] = t[2]
                op["cwaits"] = best
                for x, i in best.items():
                    widx[x] = i
                    self.ops[x][i]["sig"] = True
        sem = {}
        for e in ENG:
            c = 0
            for op in self.ops[e]:
                if op["sig"] and op["dma"] is None:
                    c += 1
                    op["sv"] = c
            sem[("c", e)] = stack.enter_context(nc.semaphore("s_" + e))
        for i, k in enumerate(self.dma_n):
            sem[("d", k)] = stack.enter_context(nc.semaphore("dq%d" % i))
        block = stack.enter_context(nc.Block())
        ops = self.ops

        def mk(en):
            def body(e):
                waited = {}
                for op in ops[en]:
                    need = {}
                    for x, i in op["cwaits"].items():
                        need[("c", x)] = ops[x][i]["sv"]
                    for t in op["deps"]:
                        if t[0] == "d":
                            s = ("d", t[1])
                            v = (1 if t[1] == "cc" else 16) * t[2]
                            if v > need.get(s, 0):
                                need[s] = v
                    for s, v in need.items():
                        if waited.get(s, 0) < v:
                            e.wait_ge(sem[s], v)
                            waited[s] = v
                    if op["fn"] is None:
                        continue
                    ins = op["fn"](e)
                    if op["dma"] == "cc":
                        ins.then_inc(sem[("d", "cc")])
                    elif op["dma"] is not None:
                        ins.then_inc(sem[("d", op["dma"])], 16)
                    elif op["sig"]:
                        ins.then_inc(sem[("c", en)], 1)
            return body

        block.tensor(mk("pe"))
        block.scalar(mk("act"))
        block.vector(mk("dve"))
        block.gpsimd(mk("pool"))
        block.sync(mk("sp"))


class _Tap(Exception):
    pass


def build_program():
    nc = bass.Bass("TRN2", target_bir_lowering=False)
    P = Prog()
    stack = ExitStack()
    try:
        _build(nc, P, stack)
    except _Tap:
        pass
    P.add("sp", None)
    P.ops["sp"][-1]["deps"] |= set(P.dma_last.values())
    P.emit(nc, stack)
    stack.close()
    return nc


def _build(nc, P, stack):
    def din(name, shape):
        return nc.dram_tensor(name, list(shape), F32, kind="ExternalInput").ap()

    xT = din("xT", [128, 8 * TOK])
    outT = nc.dram_tensor("outT", [128, 8 * TOK], F32, kind="ExternalOutput").ap()
    wgu = [din("wgu1", [NF * 128, 2 * 8 * 128]), din("wgu2", [NF * 128, 2 * 8 * 128])]
    wdn = [din("wdn1", [8 * 128, NF * 128]), din("wdn2", [8 * 128, NF * 128])]
    w_in = din("w_in", [128, 8 * 1536])
    w_out = din("w_out", [128, 8 * 1024])
    w_glu = din("w_glu", [128, 4 * 512])
    vecs = din("vecs", [128, 128])
    convw = din("convw", [128, 4 * 31])
    ssm_small = din("ssm_small", [128, 3 * 32 + 16 + 32])
    ssm_bc = din("ssm_bc", [128, 4 * 512])
    consts = din("consts", [128, 128 * 3 + 8])
    selc = din("selc", [128, 8 * 240])
    hsc = nc.dram_tensor("hsc", [128, 8 * TOK], F32).ap()
    ccin = nc.dram_tensor("ccin", [128, 128], F32)
    ccout = nc.dram_tensor("ccout", [NCORES * 128, 128], F32)

    ARENA_BYTES = 204 * 1024
    arena = stack.enter_context(nc.sbuf_tensor("arena", [128, ARENA_BYTES // 2], BF16))
    state = {"off": 0}

    def alloc(shape, dt):
        n = 1
        for s in shape:
            n *= s
        nbytes = n * (4 if dt == F32 else 2)
        nbytes = (nbytes + 63) // 64 * 64
        off = state["off"]
        state["off"] = off + nbytes
        assert state["off"] <= ARENA_BYTES, ("SBUF arena overflow", state["off"])
        ap = arena[:, off // 2:(off + nbytes) // 2]
        if dt == F32:
            ap = ap.bitcast(F32)
        ap = ap[:, 0:n]
        if len(shape) == 2:
            ap = ap.rearrange("p (a b) -> p a b", a=shape[0])
        elif len(shape) == 3:
            ap = ap.rearrange("p (a b c) -> p a b c", a=shape[0], b=shape[1])
        elif len(shape) == 4:
            ap = ap.rearrange("p (a b c d) -> p a b c d", a=shape[0], b=shape[1], c=shape[2])
        return ap

    banks = [stack.enter_context(nc.psum_tensor("bank%d" % i, [128, 512], F32)) for i in range(8)]
    bstate = {"i": 0}

    def bank():
        i = bstate["i"]
        bstate["i"] = (i + 1) % 8
        return banks[i][:, :], ("bank", i)

    vec = alloc([128], F32)
    cw = alloc([4, 31], F32)
    cst = alloc([128 * 3 + 8], F32)
    ident = cst[:, 0:128]
    maskF = cst[:, 128:256]
    maskB = cst[:, 256:384]
    pmask = cst[:, 384:392]
    onesb = alloc([128], BF16)
    blkb = alloc([128], BF16)
    negpi = alloc([1], F32)
    epsb = alloc([1], F32)
    GT_OFF = state["off"]
    gT = alloc([4, TOK + 32], BF16)
    UT_OFF = state["off"]
    uT = alloc([4, TOK], BF16)
    MARK0 = state["off"]

    VOFF = {}
    _o = 0
    for nm, ln in (("f1pre", 8), ("f1post", 8), ("mpre", 8), ("mpost", 8), ("f2pre", 8), ("f2post", 8),
                   ("b_in", 12), ("conv_b", 4), ("ln_g", 4), ("ln_b", 4), ("cout_g", 4), ("sout_g", 4),
                   ("glu_b", 4), ("b_out", 8)):
        VOFF[nm] = _o
        _o += ln

    def V(nm, i):
        o = VOFF[nm] + i
        return vec[:, o:o + 1]

    sp_ld = lambda out, in_: (lambda e: e.dma_start(out=out, in_=in_))
    P.add("sp", sp_ld(vec, vecs), writes=["vec"], dma="ld0")
    P.add("sp", sp_ld(cw.rearrange("p a b -> p (a b)"), convw), writes=["cw"], dma="ld1")
    P.add("sp", sp_ld(cst, consts), writes=["cst"], dma="ld2")
    P.add("dve", lambda e: e.memset(onesb, 1.0), writes=["onesb"])
    P.add("dve", lambda e: e.memset(blkb, 0.0), writes=["blkb"])
    P.add("dve", lambda e: e.memset(blkb[0:64, 0:64], 1.0), reads=[], writes=["blkb"])
    P.add("dve", lambda e: e.memset(blkb[64:128, 64:128], 1.0), writes=["blkb"])
    P.add("dve", lambda e: e.memset(negpi, -PI), writes=["negpi"])
    P.add("dve", lambda e: e.memset(epsb, EPS), writes=["epsb"])
    P.add("dve", lambda e: e.memset(gT[:, :, 0:16], 0.0), writes=["gThaloL"])

    def tap(name, items):
        if DEBUG != name:
            return
        P.barrier()
        o = 0
        for i, ap in enumerate(items):
            n = ap.shape[1]
            P.add("pool", (lambda ap, o, n: lambda e: e.dma_start(out=outT[:, o:o + n], in_=ap))(ap, o, n), dma=("tap", i))
            o += n
        raise _Tap()

    def mm(ps, lhsT, rhs, start, stop, reads, psres):
        P.add("pe", lambda e: e.matmul(ps, lhsT, rhs, start=start, stop=stop), reads=reads, writes=[psres])

    def rstd_from_sumsq(ps, psres, n, inv_n, out, outres):
        P.add("act", lambda e: e.activation(out, ps[:, 0:n], AF.Sqrt, bias=epsb[:, 0:1], scale=inv_n),
              reads=[psres, "epsb"], writes=[outres])
        P.add("dve", lambda e: e.reciprocal(out, out), reads=[outres], writes=[outres])

    def rms_norm_to_bf16(src, srcres, gname, dst, dstres, rstd):
        P.add("act", lambda e: e.activation(dst, src, AF.Square), reads=[srcres], writes=[dstres])
        for sb in range(NS):
            ts = slice(sb * CH, (sb + 1) * CH)
            ps, pr = bank()
            for k in range(8):
                mm(ps, onesb, dst[:, k, ts], k == 0, k == 7, [dstres, "onesb"], pr)
            rstd_from_sumsq(ps, pr, CH, 1.0 / D, rstd[:, ts], "rstd")
        for k in range(8):
            P.add("dve", (lambda k: lambda e: e.scalar_tensor_tensor(
                dst[:, k, :], src[:, k, :], V(gname, k), rstd, ALU.mult, ALU.mult))(k),
                reads=[srcres, "rstd", "vec"], writes=[dstres])

    def ffn_chunk(which, hc, hres, bufs, pre, post):
        xn, act, fo, sg, rstd, ring = bufs[:6]
        rms_norm_to_bf16(hc, hres, pre, xn, "xn", rstd)
        for f in range(NF):
            slot = ring["n"] % ring["depth"]
            ring["n"] += 1
            wt = ring["buf"][:, slot, 0:2048].rearrange("p (a k m) -> p a k m", a=2, k=8)
            src = wgu[which][f * 128:(f + 1) * 128, :]
            P.add("pool", (lambda wt, src: lambda e: e.dma_start(out=wt.rearrange("p a k m -> p (a k m)"), in_=src))(wt, src),
                  writes=[("ring", slot)], dma=("ring", slot))
            for sb in range(NS):
                ts = slice(sb * CH, (sb + 1) * CH)
                psg, prg = bank()
                psu, pru = bank()
                for k in range(8):
                    mm(psg, wt[:, 0, k, :], xn[:, k, ts], k == 0, k == 7, [("ring", slot), "xn"], prg)
                for k in range(8):
                    mm(psu, wt[:, 1, k, :], xn[:, k, ts], k == 0, k == 7, [("ring", slot), "xn"], pru)
                sgb = sg[:, sb, :]
                P.add("act", (lambda psg, sgb: lambda e: e.activation(sgb, psg, AF.Silu))(psg, sgb), reads=[prg], writes=[("sg", sb)])
                P.add("dve", (lambda f, psu, ts, sgb: lambda e: e.tensor_tensor(act[:, f, ts], sgb, psu, ALU.mult))(f, psu, ts, sgb),
                      reads=[("sg", sb), pru], writes=[("act", f)])
        for m in range(8):
            slot = ring["n"] % ring["depth"]
            ring["n"] += 1
            wt = ring["buf"][:, slot, 0:NF * 128].rearrange("p (f n) -> p f n", f=NF)
            src = wdn[which][m * 128:(m + 1) * 128, :]
            P.add("pool", (lambda wt, src: lambda e: e.dma_start(out=wt.rearrange("p f n -> p (f n)"), in_=src))(wt, src),
                  writes=[("ring", slot)], dma=("ring", slot))
            for sb in range(NS):
                ts = slice(sb * CH, (sb + 1) * CH)
                ps, pr = bank()
                for f in range(NF):
                    mm(ps, wt[:, f, :], act[:, f, ts], f == 0, f == NF - 1, [("ring", slot), ("act", f)], pr)
                P.add("act", (lambda m, ps, ts: lambda e: e.activation(fo[:, m, ts], ps, AF.Copy))(m, ps, ts), reads=[pr], writes=[("fo", m)])
        P.add("act", lambda e: e.activation(xn, fo, AF.Square), reads=[("fo", m) for m in range(8)], writes=["xn"])
        for sb in range(NS):
            ts = slice(sb * CH, (sb + 1) * CH)
            ps, pr = bank()
            for k in range(8):
                mm(ps, onesb, xn[:, k, ts], k == 0, k == 7, ["xn", "onesb"], pr)
            rstd_from_sumsq(ps, pr, CH, 1.0 / D, rstd[:, ts], "rstd")
        for k in range(8):
            P.add("dve", (lambda k: lambda e: e.scalar_tensor_tensor(
                fo[:, k, :], fo[:, k, :], V(post, k), rstd, ALU.mult, ALU.mult))(k),
                reads=[("fo", k), "rstd", "vec"], writes=[("fo", k)])
            P.add("dve", (lambda k: lambda e: e.scalar_tensor_tensor(
                hc[:, k, :], fo[:, k, :], 0.5, hc[:, k, :], ALU.mult, ALU.add))(k),
                reads=[("fo", k), hres], writes=[hres])

    def ffn_bufs():
        xn = alloc([8, SC], BF16)
        act_flat = alloc([NF * SC], BF16)
        act = act_flat.rearrange("p (f t) -> p f t", f=NF)
        fo = alloc([8, SC], F32)
        sg = alloc([NS, CH], F32)
        rstd = alloc([SC], F32)
        depth = 4
        ring = {"buf": alloc([depth, NF * 128], BF16), "n": 0, "depth": depth}
        return (xn, act, fo, sg, rstd, ring, act_flat)

    bufs = ffn_bufs()
    xn, act, fo, sg, rstd, ring, act_flat = bufs
    hb1 = alloc([8, SC], F32)
    valt = alloc([CH], F32)
    sgw = alloc([CH], F32)
    win = act_flat[:, 0:8 * 1536].rearrange("p (k n) -> p k n", k=8)
    actres = [("act", f) for f in range(NF)]
    xT3 = xT.rearrange("p (k t) -> p k t", k=8)
    hsc3 = hsc.rearrange("p (k t) -> p k t", k=8)
    outT3 = outT.rearrange("p (k t) -> p k t", k=8)
    for sc in range(NSC):
        hc = hb1
        hres = "hb"
        P.add("sp", (lambda sc: lambda e: e.dma_start(out=hb1, in_=xT3[:, :, sc * SC:(sc + 1) * SC]))(sc),
              writes=[hres], dma="hld")
        ffn_chunk(0, hc, hres, bufs, "f1pre", "f1post")
        rms_norm_to_bf16(hc, hres, "mpre", xn, "xn", rstd)
        P.add("sp", (lambda sc: lambda e: e.dma_start(out=hsc3[:, :, sc * SC:(sc + 1) * SC], in_=hb1))(sc),
              reads=[hres], writes=["hsc"], dma="hst")
        P.add("pool", lambda e: e.dma_start(out=win.rearrange("p a b -> p (a b)"), in_=w_in), writes=actres, dma="win")
        for sb in range(NS):
            c = sc * NS + sb
            ts = slice(sb * CH, (sb + 1) * CH)
            for T in range(4):
                for part in range(3):
                    m = part * 4 + T
                    ps, pr = bank()
                    for k in range(8):
                        mm(ps, win[:, k, m * 128:(m + 1) * 128], xn[:, k, ts], k == 0, k == 7, actres + ["xn"], pr)
                    if part == 0:
                        P.add("act", (lambda ps, m: lambda e: e.activation(valt, ps, AF.Identity, bias=V("b_in", m)))(ps, m),
                              reads=[pr, "vec"], writes=["valt"])
                    elif part == 1:
                        P.add("act", (lambda ps, m: lambda e: e.activation(sgw, ps, AF.Sigmoid, bias=V("b_in", m)))(ps, m),
                              reads=[pr, "vec"], writes=["sgw"])
                        P.add("dve", (lambda T, c: lambda e: e.tensor_tensor(
                            gT[:, T, 16 + c * CH:16 + (c + 1) * CH], valt, sgw, ALU.mult))(T, c),
                            reads=["valt", "sgw"], writes=[("gT", c)])
                    else:
                        P.add("act", (lambda ps, m, T, c: lambda e: e.activation(
                            uT[:, T, c * CH:(c + 1) * CH], ps, AF.Identity, bias=V("b_in", m)))(ps, m, T, c),
                            reads=[pr, "vec"], writes=[("uT", c)])
    P.barrier()
    state["off"] = MARK0

    tap("p1", [gT.rearrange("p a b -> p (a b)"), uT[:, 0:3].rearrange("p a b -> p (a b)")])

    mixcat = alloc([8, TOK], BF16)
    MARK1 = state["off"]
    W1 = alloc([32, 2, 128], BF16)
    W3 = alloc([32, 2, 128], BF16)
    Km = alloc([32, 128], BF16)
    sel = alloc([8, 240], BF16)
    A2r = alloc([2, 32], F32)
    A2i = alloc([2, 32], F32)
    MARK2 = state["off"]
    ss = alloc([3 * 32 + 16 + 32], F32)
    bc = alloc([4, 32, 16], F32)
    lamre, lamim, lstep = ss[:, 0:32], ss[:, 32:64], ss[:, 64:96]
    k1t, k2t, dskip = ss[:, 96:104], ss[:, 104:112], ss[:, 112:144]
    P.add("sp", sp_ld(ss, ssm_small), writes=["ss"], dma="ld0")
    P.add("sp", sp_ld(bc.rearrange("p a g h -> p (a g h)"), ssm_bc), writes=["bc"], dma="ld1")
    P.add("pool", lambda e: e.dma_start(out=sel.rearrange("p a b -> p (a b)"), in_=selc), writes=["sel"], dma="win")
    cnt = {"n": 0}

    def tmp(shape):
        return alloc(shape, F32)

    def dv(fn, reads, writes):
        P.add("dve", fn, reads=reads, writes=writes)

    def ac(fn, reads, writes):
        P.add("act", fn, reads=reads, writes=writes)

    def new(shape):
        cnt["n"] += 1
        return tmp(shape), ("t", cnt["n"])

    def tt(a, ra, b, rb, op, shape):
        o, ro = new(shape)
        dv(lambda e: e.tensor_tensor(o, a, b, op), [ra, rb], [ro])
        return o, ro

    def sincos(x, rx, shape):
        MAGIC = 12582912.0
        outs = []
        for shift in (0.0, PI / 2):
            xs, rxs = new(shape)
            dv((lambda xs, shift: lambda e: e.tensor_scalar(xs, x, shift, None, ALU.add))(xs, shift), [rx], [rxs])
            t, rt = new(shape)
            dv((lambda t, xs: lambda e: e.tensor_scalar(t, xs, 1.0 / (2 * PI), MAGIC, ALU.mult, ALU.add))(t, xs), [rxs], [rt])
            dv((lambda t: lambda e: e.tensor_scalar(t, t, -MAGIC, None, ALU.add))(t), [rt], [rt])
            dv((lambda t, xs: lambda e: e.scalar_tensor_tensor(t, t, -2 * PI, xs, ALU.mult, ALU.add))(t, xs), [rt, rxs], [rt])
            o, ro = new(shape)
            ac((lambda o, t: lambda e: e.activation(o, t, AF.Sin))(o, t), [rt], [ro])
            outs.append((o, ro))
        return outs

    def cexp(rho_k, rrk, th_k, rtk, shape):
        mag, rm = new(shape)
        ac(lambda e: e.activation(mag, rho_k, AF.Exp), [rrk], [rm])
        (s, rs), (c, rc) = sincos(th_k, rtk, shape)
        re, rre = tt(mag, rm, c, rc, ALU.mult, shape)
        im, rim = tt(mag, rm, s, rs, ALU.mult, shape)
        return re, rre, im, rim

    step, rstep = new([32])
    ac(lambda e: e.activation(step, lstep, AF.Exp), ["ss"], [rstep])
    rho, rrho = tt(lamre, "ss", step, rstep, ALU.mult, [32])
    th, rth = tt(lamim, "ss", step, rstep, ALU.mult, [32])
    a_r, ra_r, a_i, ra_i = cexp(rho, rrho, th, rth, [32])
    nr, rnr = new([32])
    dv(lambda e: e.tensor_scalar(nr, a_r, -1.0, None, ALU.add), [ra_r], [rnr])
    d1, rd1 = tt(lamre, "ss", lamre, "ss", ALU.mult, [32])
    d2, rd2 = tt(lamim, "ss", lamim, "ss", ALU.mult, [32])
    den, rden = tt(d1, rd1, d2, rd2, ALU.add, [32])
    dv(lambda e: e.reciprocal(den, den), [rden], [rden])
    n1, rn1 = tt(nr, rnr, lamre, "ss", ALU.mult, [32])
    n2, rn2 = tt(a_i, ra_i, lamim, "ss", ALU.mult, [32])
    numr, rnumr = tt(n1, rn1, n2, rn2, ALU.add, [32])
    n3, rn3 = tt(a_i, ra_i, lamre, "ss", ALU.mult, [32])
    n4, rn4 = tt(nr, rnr, lamim, "ss", ALU.mult, [32])
    numi, rnumi = tt(n3, rn3, n4, rn4, ALU.subtract, [32])
    cfr, rcfr = tt(numr, rnumr, den, rden, ALU.mult, [32])
    cfi, rcfi = tt(numi, rnumi, den, rden, ALU.mult, [32])
    bre, bim, cre, cim = bc[:, 0], bc[:, 1], bc[:, 2], bc[:, 3]
    S3 = [32, 16]
    bch = lambda v: v.unsqueeze(2).to_broadcast([128, 32, 16])
    q1, r1 = tt(bch(cfr), rcfr, bre, "bc", ALU.mult, S3)
    q2, r2 = tt(bch(cfi), rcfi, bim, "bc", ALU.mult, S3)
    bbr, rbbr = tt(q1, r1, q2, r2, ALU.subtract, S3)
    q3, r3 = tt(bch(cfr), rcfr, bim, "bc", ALU.mult, S3)
    q4, r4 = tt(bch(cfi), rcfi, bre, "bc", ALU.mult, S3)
    bbi, rbbi = tt(q3, r3, q4, r4, ALU.add, S3)
    rho8, rrho8 = new([32])
    dv(lambda e: e.tensor_scalar(rho8, rho, 8.0, None, ALU.mult), [rrho], [rrho8])
    th8, rth8 = new([32])
    dv(lambda e: e.tensor_scalar(th8, th, 8.0, None, ALU.mult), [rth], [rth8])
    A8r, rA8r, A8i, rA8i = cexp(rho8, rrho8, th8, rth8, [32])
    dv(lambda e: e.tensor_copy(A2r[:, 0, :], A8r), [rA8r], ["A2"])
    dv(lambda e: e.tensor_copy(A2r[:, 1, :], A8r), [rA8r], ["A2"])
    dv(lambda e: e.tensor_scalar(A2i[:, 0, :], A8i, -1.0, None, ALU.mult), [rA8i], ["A2"])
    dv(lambda e: e.tensor_copy(A2i[:, 1, :], A8i), [rA8i], ["A2"])
    S8 = [32, 8]
    bg8 = lambda v: v.unsqueeze(2).to_broadcast([128, 32, 8])
    bk8 = lambda v: v.unsqueeze(1).to_broadcast([128, 32, 8])

    def powtab(kt):
        rk, rrk = tt(bg8(rho), rrho, bk8(kt), "ss", ALU.mult, S8)
        tk, rtk = tt(bg8(th), rth, bk8(kt), "ss", ALU.mult, S8)
        return cexp(rk, rrk, tk, rtk, S8)

    P1r, rP1r, P1i, rP1i = powtab(k1t)
    P2r, rP2r, P2i, rP2i = powtab(k2t)
    S4 = [16, 8, 16]
    _save = state["off"]
    state["off"] = MARK0
    e1, re1 = new(S4)
    e2, re2 = new(S4)
    Er, rEr = new(S4)
    Ei, rEi = new(S4)
    state["off"] = _save
    F2r, rF2r = new(S4)
    F2n, rF2n = new(S4)
    kt1, rkt1 = new([128])
    kt2, rkt2 = new([128])
    Erf = Er.rearrange("p g s h -> p g (s h)")
    Eif = Ei.rearrange("p g s h -> p g (s h)")
    F2rf = F2r.rearrange("p g s h -> p g (s h)")
    F2nf = F2n.rearrange("p g s h -> p g (s h)")
    for gh in range(2):
        gs = slice(16 * gh, 16 * gh + 16)
        bp = lambda v: v[:, gs].unsqueeze(3).to_broadcast([128, 16, 8, 16])
        bb = lambda v: v[:, gs].unsqueeze(2).to_broadcast([128, 16, 8, 16])
        ba = lambda v: v[:, gs].unsqueeze(2).unsqueeze(3).to_broadcast([128, 16, 8, 16])
        W3v = W3[:, gs].rearrange("p g r (j h) -> p g r j h", j=8)
        def TT(o, x, y, op, reads, writes):
            dv(lambda e: e.tensor_tensor(o, x, y, op), reads, writes)
        TT(e1, bp(P1r), bb(bbr), ALU.mult, [rP1r, rbbr], [re1])
        TT(e2, bp(P1i), bb(bbi), ALU.mult, [rP1i, rbbi], [re2])
        TT(Er, e1, e2, ALU.subtract, [re1, re2], [rEr])
        TT(e1, bp(P1r), bb(bbi), ALU.mult, [rP1r, rbbi], [re1])
        TT(e2, bp(P1i), bb(bbr), ALU.mult, [rP1i, rbbr], [re2])
        TT(Ei, e1, e2, ALU.add, [re1, re2], [rEi])
        TT(e1, bb(cre), bp(P2r), ALU.mult, ["bc", rP2r], [re1])
        TT(e2, bb(cim), bp(P2i), ALU.mult, ["bc", rP2i], [re2])
        TT(F2r, e1, e2, ALU.subtract, [re1, re2], [rF2r])
        TT(e1, bb(cre), bp(P2i), ALU.mult, ["bc", rP2i], [re1])
        TT(e2, bb(cim), bp(P2r), ALU.mult, ["bc", rP2r], [re2])
        dv(lambda e: e.scalar_tensor_tensor(F2n, e1, -1.0, e2, ALU.mult, ALU.subtract), [re1, re2], [rF2n])
        TT(e1, F2r, ba(A8r), ALU.mult, [rF2r, rA8r], [re1])
        TT(e2, F2n, ba(A8i), ALU.mult, [rF2n, rA8i], [re2])
        TT(W3v[:, :, 0], e1, e2, ALU.add, [re1, re2], ["W3"])
        TT(e1, F2n, ba(A8r), ALU.mult, [rF2n, rA8r], [re1])
        TT(e2, F2r, ba(A8i), ALU.mult, [rF2r, rA8i], [re2])
        TT(W3v[:, :, 1], e1, e2, ALU.subtract, [re1, re2], ["W3"])
        for g16 in range(16):
            g = 16 * gh + g16
            psA, prA = bank()
            psB, prB = bank()
            for (ps, pr, lo) in ((psA, prA, 0), (psB, prB, 64)):
                P.add("pe", (lambda ps, lo, g16: lambda e: e.matmul(ps[:, 0:128], Erf[lo:lo + 64, g16, :], F2rf[lo:lo + 64, g16, :], start=True, stop=False))(ps, lo, g16),
                      reads=[rEr, rF2r], writes=[pr])
                P.add("pe", (lambda ps, lo, g16: lambda e: e.matmul(ps[:, 0:128], Eif[lo:lo + 64, g16, :], F2nf[lo:lo + 64, g16, :], start=False, stop=True))(ps, lo, g16),
                      reads=[rEi, rF2n], writes=[pr])
            dv((lambda psA: lambda e: e.tensor_tensor(kt1, psA[:, 0:128], maskF, ALU.mult))(psA), [prA, "cst"], [rkt1])
            dv((lambda psB: lambda e: e.tensor_tensor(kt2, psB[:, 0:128], maskB, ALU.mult))(psB), [prB, "cst"], [rkt2])
            dv(lambda e: e.tensor_tensor(kt1, kt1, kt2, ALU.add), [rkt1, rkt2], [rkt1])
            dv((lambda g: lambda e: e.scalar_tensor_tensor(Km[:, g, :], ident, dskip[:, g:g + 1], kt1, ALU.mult, ALU.add))(g),
               [rkt1, "cst", "ss"], ["Km"])
            for ri, (Ef, rE) in enumerate(((Erf, rEr), (Eif, rEi))):
                ps, pr = bank()
                P.add("pe", (lambda ps, Ef, g16: lambda e: e.transpose(ps[:, 0:128], Ef[:, g16, :], ident))(ps, Ef, g16),
                      reads=[rE, "cst"], writes=[pr])
                ac((lambda ps, g, ri: lambda e: e.activation(W1[:, g, ri, :], ps[:, 0:128], AF.Copy))(ps, g, ri), [pr], ["W1"])
    tap("setup", [W1[:, 0:16].rearrange("p g r m -> p (g r m)"), W3[:, 16:32].rearrange("p g r m -> p (g r m)"), Km.rearrange("p g m -> p (g m)"),
                  A2r.rearrange("p a b -> p (a b)"), A2i.rearrange("p a b -> p (a b)")])
    P.barrier()
    state["off"] = MARK2

    SX = alloc([2, 32, NB + 1], F32)
    U8 = alloc([32, NB], BF16)
    P.add("dve", lambda e: e.memset(SX[0:64, :, :, 0], 0.0), writes=[("SXF", -1)])
    ei = 0
    for g in range(32):
        ps, pr = bank()
        T, gl = g // 8, g % 8
        for j in range(8):
            mm(ps[:, 0:NB], sel[:, gl, (7 - j) * 16:(7 - j) * 16 + 128], uT[:, T, j::8], j == 0, j == 7,
               ["sel"] + [("uT", c) for c in range(NCH)], pr)
        P.add("act", (lambda ps, g: lambda e: e.activation(U8[:, g, :], ps[:, 0:NB], AF.Copy))(ps, g), reads=[pr], writes=[("U8", g)])
        for ri in range(2):
            ps2, pr2 = bank()
            mm(ps2[:, 0:NB], W1[:, g, ri, :], U8[:, g, :], True, True, ["W1", ("U8", g)], pr2)
            P.add("dve", (lambda ps2, g, ri: lambda e: e.tensor_copy(SX[0:64, ri, g, 1:NB + 1], ps2[0:64, 0:NB]))(ps2, g, ri),
                  reads=[pr2], writes=[("SXF", g)])
            P.add("act", (lambda ps2, g, ri: lambda e: e.activation(SX[64:128, ri, g, 0:NB], ps2[64:128, 0:NB], AF.Copy))(ps2, g, ri),
                  reads=[pr2], writes=[("SXB", g)])
    _save = state["off"]
    state["off"] = UT_OFF
    Y8 = alloc([8, NB], BF16)
    xb = [alloc([2, NB + 1], BF16), alloc([2, NB + 1], BF16)]
    sc1 = alloc([2, 32], F32)
    sc2 = alloc([2, 32], F32)
    pay = alloc([128], F32)
    gat = alloc([8, 128], F32)
    acc = alloc([128], F32)
    gtmp = alloc([NB], F32)
    gtmp2 = alloc([NB], F32)
    assert state["off"] <= UT_OFF + 4 * TOK * 2
    state["off"] = _save
    P.add("dve", lambda e: e.memset(Y8, 0.0), writes=[("uT", c) for c in range(NCH)] + [("Y8", gl) for gl in range(8)]
          + [("xb", 0), ("xb", 1), "sc1", "sc2", "pay", "gatA", "gatB", "gatC", "acc", "gtmp", "gtmp2"])
    allF = [("SXF", g) for g in range(-1, 32)]
    allB = [("SXB", g) for g in range(-1, 32)]
    P.add("dve", lambda e: e.memset(sc1, 0.0), reads=allF + ["A2"], writes=["scanF", "sc1"])
    for b in range(NB):
        cur = SX[0:64, :, :, b]
        cursw = SX[0:64, ::-1, :, b]
        nxt = SX[0:64, :, :, b + 1]
        dv((lambda cur: lambda e: e.tensor_tensor(sc1[0:64], A2r[0:64], cur, ALU.mult))(cur), ["scanF", "A2"], ["sc1"])
        dv((lambda cursw: lambda e: e.tensor_tensor(sc2[0:64], A2i[0:64], cursw, ALU.mult))(cursw), ["scanF", "A2"], ["sc2"])
        dv(lambda e: e.tensor_tensor(sc1[0:64], sc1[0:64], sc2[0:64], ALU.add), ["sc1", "sc2"], ["sc1"])
        dv((lambda nxt: lambda e: e.tensor_tensor(nxt, nxt, sc1[0:64], ALU.add))(nxt), ["sc1", "scanF"], ["scanF"])
    tap("scan1", [SX[:, 0, 0, :], SX[:, 1, 0, :], SX[:, 0, 31, :], SX[:, 1, 31, :], U8[:, 0, :], U8[:, 31, :]])
    P.add("dve", lambda e: e.memset(pay, 0.0), writes=["pay"])
    P.add("dve", lambda e: e.tensor_copy(pay[0:64, 0:64].rearrange("p (r g) -> p r g", r=2), SX[0:64, :, :, NB]),
          reads=["scanF"], writes=["pay"])
    P.add("dve", lambda e: e.tensor_copy(pay[:, 64:128].rearrange("p (t i) -> p t i", t=4), gT[:, :, 16 + TOK - 16:16 + TOK]),
          reads=[("gT", NCH - 1)], writes=["pay"])
    P.add("pool", lambda e: e.dma_start(out=ccin[64:128, 0:64], in_=pay[0:64, 0:64]), reads=["pay"], writes=["ccinA"], dma="cc1")
    P.add("pool", lambda e: e.dma_start(out=ccin[:, 64:128], in_=pay[:, 64:128]), reads=["pay"], writes=["ccinB"], dma="cc2")
    P.add("pool", lambda e: e.dma_start(out=ccin[0:64, 0:64], in_=pay[64:128, 0:64]), reads=["pay"], writes=["ccinC"], dma="cc3")
    P.add("pool", lambda e: e.collective_compute(
        "AllGather", ALU.bypass, replica_groups=[list(range(NCORES))],
        ins=[ccin.ap().opt()], outs=[ccout.ap().opt()]), reads=["ccinA", "ccinB", "ccinC"], writes=["ccout"], dma="cc")
    ccv = ccout.ap().rearrange("(r p) c -> p r c", p=128)
    P.add("pool", lambda e: e.dma_start(out=gat, in_=ccv), reads=["ccout"], writes=["gatA", "gatB", "gatC"], dma="cc1")
    P.add("dve", lambda e: e.tensor_scalar(acc, gat[:, 0, :], pmask[:, 0:1], None, ALU.mult), reads=["gatA", "gatB", "gatC", "cst"], writes=["acc"])
    for r in range(1, 8):
        P.add("dve", (lambda r: lambda e: e.scalar_tensor_tensor(acc, gat[:, r, :], pmask[:, r:r + 1], acc, ALU.mult, ALU.add))(r),
              reads=["acc", "gatA", "gatB", "gatC", "cst"], writes=["acc"])
    P.add("dve", lambda e: e.tensor_copy(SX[64:128, :, :, NB], acc[64:128, 0:64].rearrange("p (r g) -> p r g", r=2)),
          reads=["acc"], writes=[("SXB", -1)])
    P.add("dve", lambda e: e.tensor_copy(gT[:, :, 16 + TOK:16 + TOK + 16], acc[:, 64:128].rearrange("p (t i) -> p t i", t=4)[:, :, ::-1]),
          reads=["acc"], writes=["gThaloR"])
    tap("xchg", [acc, gT[:, 0, TOK:TOK + 32], gat.rearrange("p a b -> p (a b)")])
    P.add("dve", lambda e: e.memset(sc2, 0.0), reads=allB + ["A2"], writes=["scanB", "sc2"])
    for b in range(NB - 1, -1, -1):
        cur = SX[64:128, :, :, b + 1]
        cursw = SX[64:128, ::-1, :, b + 1]
        nxt = SX[64:128, :, :, b]
        dv((lambda cur: lambda e: e.tensor_tensor(sc1[64:128], A2r[64:128], cur, ALU.mult))(cur), ["scanB", "A2"], ["sc1"])
        dv((lambda cursw: lambda e: e.tensor_tensor(sc2[64:128], A2i[64:128], cursw, ALU.mult))(cursw), ["scanB", "A2"], ["sc2"])
        dv(lambda e: e.tensor_tensor(sc1[64:128], sc1[64:128], sc2[64:128], ALU.add), ["sc1", "sc2"], ["sc1"])
        dv((lambda nxt: lambda e: e.tensor_tensor(nxt, nxt, sc1[64:128], ALU.add))(nxt), ["sc1", "scanB"], ["scanB"])
    tap("scan2", [SX[:, 0, 0, :], SX[:, 1, 0, :], SX[:, 0, 31, :], SX[:, 1, 31, :]])
    for T in range(4):
        for gl in range(8):
            g = T * 8 + gl
            x = xb[g % 2]
            xr = ("xb", g % 2)
            P.add("act", (lambda x, g: lambda e: e.activation(x[0:64, :, 0:NB], SX[0:64, :, g, 0:NB], AF.Copy))(x, g), reads=["scanF"], writes=[xr])
            P.add("act", (lambda x, g: lambda e: e.activation(x[64:128, :, 0:NB], SX[64:128, :, g, 1:NB + 1], AF.Copy))(x, g), reads=["scanB"], writes=[xr])
            ps, pr = bank()
            mm(ps[:, 0:NB], Km[:, g, :], U8[:, g, :], True, False, ["Km", ("U8", g)], pr)
            for ri in range(2):
                mm(ps[:, 0:NB], W3[:, g, ri, :], x[:, ri, 0:NB], False, ri == 1, ["W3", xr], pr)
            P.add("act", (lambda ps, gl: lambda e: e.activation(Y8[:, gl, :], ps[:, 0:NB], AF.Copy))(ps, gl), reads=[pr], writes=[("Y8", gl)])
        if T == 0:
            tap("y8", [Y8.rearrange("p a b -> p (a b)")])
        for j in range(8):
            ps, pr = bank()
            for gl in range(8):
                mm(ps[:, 0:NB], sel[:, j, (7 - gl) * 16:(7 - gl) * 16 + 128], Y8[:, gl, :], gl == 0, gl == 7, ["sel", ("Y8", gl)], pr)
            psn = ps[:, 0:NB]
            ac((lambda psn: lambda e: e.activation(gtmp, psn, AF.Square))(psn), [pr], ["gtmp"])
            dv(lambda e: e.tensor_scalar(gtmp, gtmp, 0.044715, 1.0, ALU.mult, ALU.add), ["gtmp"], ["gtmp"])
            dv((lambda psn: lambda e: e.tensor_tensor(gtmp, gtmp, psn, ALU.mult))(psn), ["gtmp", pr], ["gtmp"])
            ac(lambda e: e.activation(gtmp2, gtmp, AF.Sigmoid, scale=2.0 * math.sqrt(2.0 / PI)), ["gtmp"], ["gtmp2"])
            dv((lambda psn, T, j: lambda e: e.tensor_tensor(mixcat[:, 4 + T, j::8], gtmp2, psn, ALU.mult))(psn, T, j),
               ["gtmp2", pr], [("yT", T)])
    P.barrier()
    state["off"] = MARK2
    tap("yT", [mixcat[:, 4 + T, :] for T in range(4)])
    zt = alloc([4, CH], F32)
    sq4 = alloc([4, CH], BF16)
    rs4 = alloc([CH], F32)
    sgm = alloc([CH], F32)
    wglu2 = alloc([4, 512], BF16)
    P.add("pool", lambda e: e.dma_start(out=wglu2.rearrange("p a b -> p (a b)"), in_=w_glu), writes=["wglu2"], dma="win")
    yres = [("yT", T) for T in range(4)]
    for c in range(NCH):
        cs = slice(c * CH, (c + 1) * CH)
        for m in range(4):
            ps, pr = bank()
            for k in range(4):
                mm(ps, wglu2[:, k, m * 128:(m + 1) * 128], mixcat[:, 4 + k, cs], k == 0, k == 3, ["wglu2"] + yres, pr)
            ac((lambda ps, m: lambda e: e.activation(sgm, ps, AF.Sigmoid, bias=V("glu_b", m)))(ps, m), [pr, "vec"], ["sgm"])
            dv((lambda m, cs: lambda e: e.tensor_tensor(zt[:, m, :], mixcat[:, 4 + m, cs], sgm, ALU.mult))(m, cs), ["sgm"] + yres, [("zt", m)])
        ac(lambda e: e.activation(sq4, zt, AF.Square), [("zt", m) for m in range(4)], ["sq4"])
        for m in range(4):
            ps, pr = bank()
            mm(ps, blkb, sq4[:, m, :], True, True, ["blkb", "sq4"], pr)
            rstd_from_sumsq(ps, pr, CH, 1.0 / 64, rs4, "rs4")
            dv((lambda m, cs: lambda e: e.scalar_tensor_tensor(mixcat[:, 4 + m, cs], zt[:, m, :], V("sout_g", m), rs4, ALU.mult, ALU.mult))(m, cs),
               [("zt", m), "rs4", "vec"], yres)
    dgw = alloc([4, 31, 128], BF16)
    dbuf = alloc([4, CH], F32)
    dsq = alloc([4, CH], BF16)
    dbf = alloc([4, CH], BF16)
    mean = alloc([CH], F32)
    var = alloc([CH], F32)
    for T in range(4):
        for k in range(31):
            dv((lambda T, k: lambda e: e.tensor_scalar(dgw[:, T, k, :], ident, cw[:, T, k:k + 1], None, ALU.mult))(T, k), ["cst", "cw"], ["dgw"])
    gres = [("gT", c) for c in range(NCH)] + ["gThaloL", "gThaloR"]
    for c in range(NCH):
        cs = slice(c * CH, (c + 1) * CH)
        for T in range(4):
            ps, pr = bank()
            for k in range(31):
                o = 16 + c * CH + k - 15
                mm(ps, dgw[:, T, k, :], gT[:, T, o:o + CH], k == 0, k == 30, ["dgw"] + gres, pr)
            ac((lambda ps, T: lambda e: e.activation(dbuf[:, T, :], ps, AF.Identity, bias=V("conv_b", T)))(ps, T), [pr, "vec"], [("dbuf", T)])
        dall = [("dbuf", T) for T in range(4)]
        ac(lambda e: e.activation(dsq, dbuf, AF.Square), dall, ["dsq"])
        ac(lambda e: e.activation(dbf, dbuf, AF.Copy), dall, ["dbf"])
        ps1, pr1 = bank()
        ps2, pr2 = bank()
        for T in range(4):
            mm(ps1, onesb, dbf[:, T, :], T == 0, T == 3, ["onesb", "dbf"], pr1)
        for T in range(4):
            mm(ps2, onesb, dsq[:, T, :], T == 0, T == 3, ["onesb", "dsq"], pr2)
        dv((lambda ps1: lambda e: e.tensor_scalar(mean, ps1, 1.0 / 512, None, ALU.mult))(ps1), [pr1], ["mean"])
        dv(lambda e: e.tensor_tensor(var, mean, mean, ALU.mult), ["mean"], ["var"])
        dv((lambda ps2: lambda e: e.scalar_tensor_tensor(var, ps2, 1.0 / 512, var, ALU.mult, ALU.subtract))(ps2), [pr2, "var"], ["var"])
        ac(lambda e: e.activation(var, var, AF.Sqrt, bias=epsb[:, 0:1]), ["var", "epsb"], ["var"])
        dv(lambda e: e.reciprocal(var, var), ["var"], ["var"])
        for T in range(4):
            dv((lambda T: lambda e: e.tensor_tensor(dbuf[:, T, :], dbuf[:, T, :], mean, ALU.subtract))(T), [("dbuf", T), "mean"], [("dbuf", T)])
            dv((lambda T: lambda e: e.tensor_tensor(dbuf[:, T, :], dbuf[:, T, :], var, ALU.mult))(T), [("dbuf", T), "var"], [("dbuf", T)])
            ac((lambda T: lambda e: e.activation(dbuf[:, T, :], dbuf[:, T, :], AF.Silu, bias=V("ln_b", T), scale=V("ln_g", T)))(T),
               [("dbuf", T), "vec"], [("dbuf", T)])
        ac(lambda e: e.activation(dsq, dbuf, AF.Square), dall, ["dsq"])
        for T in range(4):
            ps, pr = bank()
            mm(ps, blkb, dsq[:, T, :], True, True, ["blkb", "dsq"], pr)
            rstd_from_sumsq(ps, pr, CH, 1.0 / 64, rs4, "rs4")
            dv((lambda T, cs: lambda e: e.scalar_tensor_tensor(mixcat[:, T, cs], dbuf[:, T, :], V("cout_g", T), rs4, ALU.mult, ALU.mult))(T, cs),
               [("dbuf", T), "rs4", "vec"], [("cy", T)])
    tap("mix", [mixcat[:, k, :] for k in range(8)])
    P.barrier()
    state["off"] = MARK1

    bufs3 = ffn_bufs()
    xn3, act3, fo3, sg3, rstd3, ring3, act3_flat = bufs3
    _save = state["off"]
    state["off"] = GT_OFF
    hb3 = alloc([8, SC], F32)
    assert state["off"] <= MARK0
    state["off"] = _save
    wout = act3_flat[:, 0:8 * 1024].rearrange("p (k n) -> p k n", k=8)
    mres = [("cy", T) for T in range(4)] + yres
    for sc in range(NSC):
        hc = hb3
        hres = "hb"
        P.add("sp", (lambda sc: lambda e: e.dma_start(out=hb3, in_=hsc3[:, :, sc * SC:(sc + 1) * SC]))(sc),
              reads=["hsc"], writes=[hres], dma="hld")
        P.add("pool", lambda e: e.dma_start(out=wout.rearrange("p a b -> p (a b)"), in_=w_out), writes=actres, dma="win")
        for m in range(8):
            for sb in range(NS):
                ts = slice(sb * CH, (sb + 1) * CH)
                cs = slice(sc * SC + sb * CH, sc * SC + (sb + 1) * CH)
                ps, pr = bank()
                for k in range(8):
                    mm(ps, wout[:, k, m * 128:(m + 1) * 128], mixcat[:, k, cs], k == 0, k == 7, actres + mres, pr)
                ac((lambda ps, m, ts: lambda e: e.activation(fo3[:, m, ts], ps, AF.Identity, bias=V("b_out", m)))(ps, m, ts), [pr, "vec"], [("fo", m)])
        ac(lambda e: e.activation(xn3, fo3, AF.Square), [("fo", m) for m in range(8)], ["xn"])
        for sb in range(NS):
            ts = slice(sb * CH, (sb + 1) * CH)
            ps, pr = bank()
            for k in range(8):
                mm(ps, onesb, xn3[:, k, ts], k == 0, k == 7, ["xn", "onesb"], pr)
            rstd_from_sumsq(ps, pr, CH, 1.0 / D, rstd3[:, ts], "rstd")
        for k in range(8):
            dv((lambda k: lambda e: e.scalar_tensor_tensor(fo3[:, k, :], fo3[:, k, :], V("mpost", k), rstd3, ALU.mult, ALU.mult))(k),
               [("fo", k), "rstd", "vec"], [("fo", k)])
            dv((lambda k: lambda e: e.tensor_tensor(hb3[:, k, :], hb3[:, k, :], fo3[:, k, :], ALU.add))(k), [("fo", k), hres], [hres])
        ffn_chunk(1, hc, hres, bufs3, "f2pre", "f2post")
        P.add("sp", (lambda sc: lambda e: e.dma_start(out=outT3[:, :, sc * SC:(sc + 1) * SC], in_=hb3))(sc), reads=[hres], dma="hst")


def _sel_const():
    m = np.zeros((128, 8, 240), np.float32)
    for gl in range(8):
        for h in range(16):
            m[gl * 16 + h, gl, 7 * 16 + h] = 1.0
    return m.reshape(128, 8 * 240)


def _host_inputs(inp):
    f = lambda a: np.ascontiguousarray(a, dtype=np.float32)
    shared = {}
    for i, s in ((1, "ffn1"), (2, "ffn2")):
        wg = inp[s + "_w_gate"][0].reshape(8, 128, NF, 128)
        wu = inp[s + "_w_up"][0].reshape(8, 128, NF, 128)
        gu = np.stack([wg, wu], 0)
        shared["wgu%d" % i] = f(gu.transpose(3, 2, 0, 1, 4).reshape(NF * 128, 2 * 8 * 128))
        wd = inp[s + "_w_down"][0].reshape(NF, 128, 8, 128)
        shared["wdn%d" % i] = f(wd.transpose(2, 1, 0, 3).reshape(8 * 128, NF * 128))
    shared["w_in"] = f(inp["w_in"][0].reshape(8, 128, 1536).transpose(1, 0, 2).reshape(128, 8 * 1536))
    shared["w_out"] = f(inp["w_out"][0].reshape(8, 128, 1024).transpose(1, 0, 2).reshape(128, 8 * 1024))
    shared["w_glu"] = f(inp["ssm_glu_w"][0].reshape(4, 128, 512).transpose(1, 0, 2).reshape(128, 4 * 512))
    cols = []
    for nm, ln in (("ffn1_pre_g", 8), ("ffn1_post_g", 8), ("mix_pre_g", 8), ("mix_post_g", 8), ("ffn2_pre_g", 8),
                   ("ffn2_post_g", 8), ("b_in", 12), ("conv_b", 4), ("conv_ln_g", 4), ("conv_ln_b", 4),
                   ("conv_out_g", 4), ("ssm_out_g", 4), ("ssm_glu_b", 4), ("b_out", 8)):
        cols.append(inp[nm][0].reshape(ln, 128).T)
    v = np.concatenate(cols, 1)
    vecs = np.zeros((128, 128), np.float32)
    vecs[:, :v.shape[1]] = v
    shared["vecs"] = vecs
    shared["selc"] = _sel_const()
    ident = np.eye(128, dtype=np.float32)
    s_idx = np.arange(128) // 16
    maskF = (s_idx[:, None] <= s_idx[None, :]).astype(np.float32)
    maskB = (s_idx[:, None] >= s_idx[None, :]).astype(np.float32)
    k1 = np.zeros((128, 8), np.float32)
    k2 = np.zeros((128, 8), np.float32)
    k1[:64] = 7 - np.arange(8)
    k1[64:] = np.arange(8)
    k2[:64] = np.arange(8) - 7
    k2[64:] = -np.arange(8)
    dsk = np.tile(inp["ssm_d"][0].reshape(32, 16).T, (8, 1))
    maps = []
    for c in range(NCORES):
        b, rev = c // 2, c % 2
        m = dict(shared)
        xs = inp["x"][b, 0:TOK] if not rev else inp["x"][b, 2 * TOK - 1:TOK - 1:-1]
        m["xT"] = f(xs.T.reshape(8, 128, TOK).transpose(1, 0, 2).reshape(128, 8 * TOK))
        cwk = inp["conv_w"][0] if not rev else inp["conv_w"][0][::-1]
        m["convw"] = f(cwk.reshape(31, 4, 128).transpose(2, 1, 0).reshape(128, 4 * 31))
        dirs = ("f", "b") if not rev else ("b", "f")
        lamre = np.concatenate([inp["lam_re_" + d][0].T for d in dirs], 0)
        lamim = np.concatenate([inp["lam_im_" + d][0].T for d in dirs], 0)
        lst = np.concatenate([np.tile(inp["log_step_" + d][0][None, :], (64, 1)) for d in dirs], 0)
        m["ssm_small"] = f(np.concatenate([lamre, lamim, lst, k1, k2, dsk], 1))
        bre = np.concatenate([inp["b_re_" + d][0].transpose(1, 0, 2) for d in dirs], 0)
        bim = np.concatenate([inp["b_im_" + d][0].transpose(1, 0, 2) for d in dirs], 0)
        cre = np.concatenate([inp["c_re_" + d][0].transpose(2, 0, 1) for d in dirs], 0)
        cim = np.concatenate([inp["c_im_" + d][0].transpose(2, 0, 1) for d in dirs], 0)
        m["ssm_bc"] = f(np.stack([bre, bim, cre, cim], 1).reshape(128, 4 * 512))
        pm = np.zeros((128, 8), np.float32)
        pm[:, c ^ 1] = 1.0
        m["consts"] = f(np.concatenate([ident, maskF, maskB, pm], 1))
        maps.append(m)
    return maps


_NC_CACHE = {}


def kernel(**inputs):
    inp = {k: np.asarray(v) for k, v in inputs.items()}
    if "nc" not in _NC_CACHE:
        _NC_CACHE["nc"] = build_program()
    nc = _NC_CACHE["nc"]
    maps = _host_inputs(inp)
    res = run_bass_kernel_spmd(nc, maps, core_ids=list(range(NCORES)))
    out = np.zeros((4, 2 * TOK, D), np.float32)
    for c in range(NCORES):
        b, rev = c // 2, c % 2
        o = np.asarray(res.results[c]["outT"]).reshape(128, 8, TOK).transpose(1, 0, 2).reshape(D, TOK).T
        if not rev:
            out[b, 0:TOK] = o
        else:
            out[b, 2 * TOK - 1:TOK - 1:-1] = o
    return out
```
